# Optimizing a Trainium2 kernel written in Bass

```python
import math
import jax, jax.numpy as jnp
from jax import lax
import numpy as np

D_MODEL = 1024
BATCH = 4
SEQ = 8192
DEPTH = 2

CHUNK = 64
N_META = 16
SB_HEADS = 8
SB_HEAD_DIM = 64
SB_WIDTH = SB_HEADS * SB_HEAD_DIM
CONV_CH = D_MODEL // 2
CONV_K = 31
D_FF = 2816
Q_BLOCK = 128
NORM_EPS = 1e-6
LN_EPS = 1e-5
IN_SIZES = [SB_WIDTH, SB_WIDTH, SB_WIDTH, CONV_CH, CONV_CH, D_MODEL, D_MODEL]
IN_SPLITS = [int(s) for s in np.cumsum(IN_SIZES)[:-1]]
IN_WIDTH = int(sum(IN_SIZES))

kernel_name = "hybrid_stickbreak_conformer_macaron"


def rmsnorm(x, g):
    xf = x.astype(jnp.float32)
    y = xf * lax.rsqrt(jnp.mean(xf * xf, axis=-1, keepdims=True) + NORM_EPS)
    return (y * g.astype(jnp.float32)).astype(x.dtype)


def layernorm(x, g, b):
    xf = x.astype(jnp.float32)
    mu = jnp.mean(xf, axis=-1, keepdims=True)
    var = jnp.mean(jnp.square(xf - mu), axis=-1, keepdims=True)
    y = (xf - mu) * lax.rsqrt(var + LN_EPS)
    return (y * g.astype(jnp.float32) + b.astype(jnp.float32)).astype(x.dtype)


def swiglu(x, w_gate, w_up, w_down):
    return (jax.nn.silu(x @ w_gate) * (x @ w_up)) @ w_down


def stick_breaking_attention(q, k, v):
    B, L, _ = q.shape
    Lp = ((L + Q_BLOCK - 1) // Q_BLOCK) * Q_BLOCK
    nb = Lp // Q_BLOCK

    def heads(a):
        a = a.reshape(B, L, SB_HEADS, SB_HEAD_DIM).transpose(0, 2, 1, 3).astype(jnp.float32)
        return jnp.pad(a, ((0, 0), (0, 0), (0, Lp - L), (0, 0)))

    qh, kh, vh = heads(q), heads(k), heads(v)
    scale = 1.0 / math.sqrt(SB_HEAD_DIM)
    q_blocks = qh.reshape(B, SB_HEADS, nb, Q_BLOCK, SB_HEAD_DIM).transpose(2, 0, 1, 3, 4)
    starts = jnp.arange(nb, dtype=jnp.int32) * Q_BLOCK
    key_pos = jnp.arange(Lp, dtype=jnp.int32)[None, :]

    def block(args):
        qb, start = args
        z = jnp.einsum('bhqd,bhkd->bhqk', qb, kh) * scale
        qpos = start + jnp.arange(Q_BLOCK, dtype=jnp.int32)[:, None]
        valid = key_pos < qpos
        log_keep = jnp.where(valid, jax.nn.log_sigmoid(-z), 0.0)
        after = lax.cumsum(log_keep, axis=3, reverse=True) - log_keep
        attn = jnp.where(valid, jnp.exp(jax.nn.log_sigmoid(z) + after), 0.0)
        return jnp.einsum('bhqk,bhkd->bhqd', attn, vh)

    out = lax.map(block, (q_blocks, starts))
    out = out.transpose(1, 2, 0, 3, 4).reshape(B, SB_HEADS, Lp, SB_HEAD_DIM)[:, :, :L]
    return out.transpose(0, 2, 1, 3).reshape(B, L, SB_WIDTH).astype(q.dtype)


def conformer_conv(a, b, conv_w, conv_b, ln_g, ln_b):
    u = a * jax.nn.sigmoid(b)
    y = lax.conv_general_dilated(
        u, conv_w[:, None, :].astype(u.dtype), window_strides=(1,),
        padding=[(CONV_K - 1, 0)],
        dimension_numbers=('NWC', 'WIO', 'NWC'), feature_group_count=CONV_CH)
    y = y + conv_b.astype(y.dtype)
    return jax.nn.silu(layernorm(y, ln_g, ln_b))


def hybrid_mixer(u, w_in, w_attn_out, conv_w, conv_b, conv_ln_g, conv_ln_b, w_conv_out, w_out):
    proj = u @ w_in
    q, k, v, a, b, g_sb, g_conv = jnp.split(proj, IN_SPLITS, axis=-1)
    y_sb = stick_breaking_attention(q, k, v) @ w_attn_out
    y_conv = conformer_conv(a, b, conv_w, conv_b, conv_ln_g, conv_ln_b) @ w_conv_out
    m = jax.nn.sigmoid(g_sb) * y_sb + jax.nn.sigmoid(g_conv) * y_conv
    return m @ w_out


def setup_inputs(seed: int = 0) -> dict:
    key = jax.random.key(seed)
    ks = iter(jax.random.split(key, 32))

    def w(shape, fan_in, scale=1.0):
        return jax.random.normal(next(ks), shape, jnp.float32) * (scale * fan_in ** -0.5)

    def gain(shape):
        return 1.0 + 0.05 * jax.random.normal(next(ks), shape, jnp.float32)

    def bias(shape):
        return 0.02 * jax.random.normal(next(ks), shape, jnp.float32)

    L_ = DEPTH
    return {
        "x": jax.random.normal(next(ks), (BATCH, SEQ, D_MODEL), jnp.float32),
        "meta": jax.random.normal(next(ks), (N_META, D_MODEL), jnp.float32),
        "ffn1_norm": gain((L_, D_MODEL)),
        "ffn1_w_gate": w((L_, D_MODEL, D_FF), D_MODEL),
        "ffn1_w_up": w((L_, D_MODEL, D_FF), D_MODEL),
        "ffn1_w_down": w((L_, D_FF, D_MODEL), D_FF),
        "mix_norm": gain((L_, D_MODEL)),
        "w_in": w((L_, D_MODEL, IN_WIDTH), D_MODEL),
        "w_attn_out": w((L_, SB_WIDTH, D_MODEL), SB_WIDTH),
        "conv_w": w((L_, CONV_K, CONV_CH), CONV_K),
        "conv_b": bias((L_, CONV_CH)),
        "conv_ln_g": gain((L_, CONV_CH)),
        "conv_ln_b": bias((L_, CONV_CH)),
        "w_conv_out": w((L_, CONV_CH, D_MODEL), CONV_CH),
        "w_out": w((L_, D_MODEL, D_MODEL), D_MODEL),
        "ffn2_norm": gain((L_, D_MODEL)),
        "ffn2_w_gate": w((L_, D_MODEL, D_FF), D_MODEL),
        "ffn2_w_up": w((L_, D_MODEL, D_FF), D_MODEL),
        "ffn2_w_down": w((L_, D_FF, D_MODEL), D_FF),
        "final_norm": gain((D_MODEL,)),
    }


def reference(x, meta, ffn1_norm, ffn1_w_gate, ffn1_w_up, ffn1_w_down, mix_norm, w_in,
              w_attn_out, conv_w, conv_b, conv_ln_g, conv_ln_b, w_conv_out, w_out,
              ffn2_norm, ffn2_w_gate, ffn2_w_up, ffn2_w_down, final_norm):
    B = x.shape[0]
    meta_b = jnp.broadcast_to(meta.astype(x.dtype)[None], (B, N_META, D_MODEL))
    h = jnp.concatenate([meta_b, x], axis=1)
    for l in range(DEPTH):
        h = h + 0.5 * swiglu(rmsnorm(h, ffn1_norm[l]), ffn1_w_gate[l], ffn1_w_up[l], ffn1_w_down[l])
        h = h + hybrid_mixer(rmsnorm(h, mix_norm[l]), w_in[l], w_attn_out[l], conv_w[l], conv_b[l],
                             conv_ln_g[l], conv_ln_b[l], w_conv_out[l], w_out[l])
        h = h + 0.5 * swiglu(rmsnorm(h, ffn2_norm[l]), ffn2_w_gate[l], ffn2_w_up[l], ffn2_w_down[l])
    h = rmsnorm(h, final_norm)
    return h[:, N_META:]
```

```python
import contextlib
import numpy as np
import concourse.bass as bass
import concourse.mybir as mybir
from concourse.bass_utils import run_bass_kernel_spmd

F32 = mybir.dt.float32
BF16 = mybir.dt.bfloat16
AF = mybir.ActivationFunctionType
ALU = mybir.AluOpType

D = 1024
DFF = 2816
NFC = 22
NH = 8
DH = 64
CCH = 512
KC = 31
NMETA = 16
NMP = 32
TN = 512
NCH_FULL = 8
NLAYER = 2
TPL = 192
NWT = TPL * NLAYER
WT_PER_CORE = NWT // 8
T_F1G, T_F1U, T_F1D, T_WIN, T_WAO, T_WCO, T_WOUT, T_F2G, T_F2U, T_F2D = 0, 22, 44, 66, 102, 110, 118, 126, 148, 170
PV_F1N, PV_MIXN, PV_F2N, PV_CB, PV_LNG, PV_LNB, PV_CW = 0, 8, 16, 24, 28, 32, 36
PV_PER_L = 36 + KC * 4
PV_FINAL = PV_PER_L * NLAYER
PV_N = PV_FINAL + 8
C_ID, C_TRI, C_MASK, C_MM = 0, 128, 256, 256 + 4 * 512
C_N = C_MM + 32
GR_K, GR_V, GR_H = 0, 512, 1024
GR = 1056


def owner(j):
    return 0 if (j % 4) in (0, 3) else 1


class Buf:
    __slots__ = ("name", "w", "r", "al")

    def __init__(self, name):
        self.name = name
        self.w = {}
        self.r = {}
        self.al = ()


def alias(a, b):
    a.al = tuple(a.al) + (b,)
    b.al = tuple(b.al) + (a,)


class Em:
    def __init__(self, nc, es):
        self.nc = nc
        self.eng = {"pe": nc.tensor, "act": nc.scalar, "dve": nc.vector, "pool": nc.gpsimd, "sp": nc.sync}
        self.sem = {}
        self.cnt = {}
        for k in ("pe", "act", "dve", "pool", "cc"):
            self.sem[k] = es.enter_context(nc.semaphore("s_" + k))
            self.cnt[k] = 0
        self.NDS = 8
        self.dq = {}
        self.dcnt = {}
        for q in ("sp", "pool"):
            self.dq[q] = []
            for i in range(self.NDS):
                key = "d_%s%d" % (q, i)
                self.sem[key] = es.enter_context(nc.semaphore(key))
                self.dq[q].append(key)
            self.dcnt[q] = 0
        self.waited = {e: {} for e in self.eng}
        self.nwait = 0
        self.nins = 0

    def _gather(self, reads, writes):
        deps = {}

        def add(d):
            for k, v in d.items():
                if deps.get(k, 0) < v:
                    deps[k] = v

        for b in reads:
            for bb in (b,) + tuple(b.al):
                add(bb.w)
        for b in writes:
            for bb in (b,) + tuple(b.al):
                add(bb.w)
                add(bb.r)
        return deps

    def _wait(self, e, deps):
        w = self.waited[e]
        for k, v in deps.items():
            if e == "pe" and k == "pe":
                continue
            if w.get(k, 0) >= v:
                continue
            self.eng[e].wait_ge(self.sem[k], v)
            w[k] = v
            self.nwait += 1

    def _record(self, tok, reads, writes, additive=False):
        k, v = tok
        for b in reads:
            if b.r.get(k, 0) < v:
                b.r[k] = v
        for b in writes:
            if additive:
                if b.w.get(k, 0) < v:
                    b.w[k] = v
            else:
                b.w = {k: v}
                b.r = {}

    def op(self, e, fn, reads=(), writes=()):
        self._wait(e, self._gather(reads, writes))
        ins = fn(self.eng[e])
        self.cnt[e] += 1
        ins.then_inc(self.sem[e], 1)
        self._record((e, self.cnt[e]), reads, writes)
        self.nins += 1

    def dma(self, q, out, in_, reads=(), writes=(), additive=False, **kw):
        deps = self._gather(reads, () if additive else writes)
        n = self.dcnt[q]
        key = self.dq[q][n % self.NDS]
        tgt = 16 * (n // self.NDS + 1)
        if tgt > 16:
            deps[key] = max(deps.get(key, 0), tgt - 16)
        self._wait(q, deps)
        ins = self.eng[q].dma_start(out=out, in_=in_, **kw)
        ins.then_inc(self.sem[key], 16)
        self.dcnt[q] = n + 1
        self._record((key, tgt), reads, writes, additive)
        self.nins += 1

    def collective(self, kind, groups, in_ap, out_ap, reads, writes):
        self._wait("pool", self._gather(reads, writes))
        ins = self.eng["pool"].collective_compute(kind, ALU.bypass, replica_groups=groups,
                                                  ins=[in_ap], outs=[out_ap])
        self.cnt["cc"] += 1
        ins.then_inc(self.sem["cc"])
        self._record(("cc", self.cnt["cc"]), reads, writes)

    def barrier_end(self, bufs):
        deps = self._gather(bufs, ())
        self._wait("sp", deps)


class Pool:
    def __init__(self, name, aps):
        self.aps = aps
        self.bufs = [Buf("%s%d" % (name, i)) for i in range(len(aps))]
        self.i = 0

    def get(self):
        i = self.i % len(self.aps)
        self.i += 1
        return self.aps[i], self.bufs[i]


def build(NCH, stage=99, coll=True):
    NT = NMP + NCH * TN
    NTOK = NCH * TN
    nc = bass.Bass("TRN2", target_bir_lowering=False)
    x_in = nc.dram_tensor("x", [NTOK, D], F32, kind="ExternalInput").ap()
    meta_in = nc.dram_tensor("meta", [NMETA, D], F32, kind="ExternalInput").ap()
    NWL = WT_PER_CORE if coll else NWT
    wsrc = nc.dram_tensor("wsrc", [NWL * 128, 1024], F32, kind="ExternalInput").ap()
    pvec_in = nc.dram_tensor("pvec", [128, PV_N], F32, kind="ExternalInput").ap()
    cst_in = nc.dram_tensor("cst", [128, C_N], F32, kind="ExternalInput").ap()
    flg_in = nc.dram_tensor("flg", [128, 24], F32, kind="ExternalInput").ap()
    y_out = nc.dram_tensor("y", [NTOK, D], F32, kind="ExternalOutput").ap()
    wpart = nc.dram_tensor("wpart", [WT_PER_CORE * 128, 1024], BF16)
    wall = nc.dram_tensor("wall", [NWT * 128, 1024], BF16)
    hT_d = nc.dram_tensor("hT_d", [D, NT], F32).ap()
    GRn = 132 * NCH
    gin = [nc.dram_tensor("gin%d" % l, [GRn, 4096], BF16) for l in range(NLAYER)]
    gout = [nc.dram_tensor("gout%d" % l, [2 * GRn, 4096], BF16) for l in range(NLAYER)]

    def gviews(t, rk):
        flat = t.ap()[rk * GRn:(rk + 1) * GRn, :].rearrange("r c -> (r c)")
        kv = flat[0:512 * NTOK].rearrange("(r t) -> r t", t=NTOK)
        vv = flat[512 * NTOK:1024 * NTOK].rearrange("(t c) -> t c", c=512)
        hv = flat[1024 * NTOK:1024 * NTOK + NCH * 16384].rearrange("(r x) -> r x", x=4096)
        return kv, vv, hv
    q_d = [nc.dram_tensor("q_d%d" % l, [512, NT], BF16).ap() for l in range(NLAYER)]
    ug_d = [nc.dram_tensor("ug_d%d" % l, [512, NT], BF16).ap() for l in range(NLAYER)]
    gate_d = [nc.dram_tensor("gate_d%d" % l, [2 * D, NT], F32).ap() for l in range(NLAYER)]
    km_d = [nc.dram_tensor("km_d%d" % l, [512, NMP], BF16).ap() for l in range(NLAYER)]
    vm_d = [nc.dram_tensor("vm_d%d" % l, [NMP, 512], BF16).ap() for l in range(NLAYER)]
    wall_ap = wall.ap()

    es = contextlib.ExitStack()
    with es:
        em = Em(nc, es)

        def sb(name, shape, dt):
            return es.enter_context(nc.sbuf_tensor(name, shape, dt))

        def mkpool(name, n, shape, dt):
            return Pool(name, [sb("%s_%d" % (name, i), shape, dt) for i in range(n)])

        identf = sb("identf", [128, 128], F32)
        tri = sb("tri", [128, 128], BF16)
        negones = sb("negones", [128, 128], BF16)
        onesD = sb("onesD", [128, 128], BF16)
        onesC = sb("onesC", [128, 128], BF16)
        masks = sb("masks", [128, 4, 512], BF16)
        mmask = sb("mmask", [128, 32], BF16)
        epsT = sb("epsT", [128, 2], F32)
        pv = sb("pv", [128, PV_N], F32)
        flg = sb("flg_sb", [128, 24], F32)
        h1t = sb("h1t", [128, 4, 32], BF16)
        h2t = sb("h2t", [128, 4, 32], BF16)
        h3t = sb("h3t", [128, 4, 32], F32)
        B_h1, B_h2, B_h3 = Buf("h1"), Buf("h2"), Buf("h3")
        B_const = Buf("const")
        hT = sb("hT", [128, 8, TN], F32)
        B_hT = [Buf("hT%d" % i) for i in range(8)]
        uT = sb("uT", [128, 8, TN], BF16)
        B_uT = [Buf("uT%d" % i) for i in range(8)]
        hid = sb("hid", [128, NFC, TN], BF16)
        B_hid = [Buf("hid%d" % i) for i in range(NFC)]
        wpool = mkpool("w", 6, [128, 1024], BF16)
        f32p = mkpool("f", 8, [128, TN], F32)
        b16p = mkpool("b", 6, [128, TN], BF16)
        stg = mkpool("stg", 2, [128, 1024], F32)
        ug = sb("ug", [128, 4, 32 + TN], BF16)
        B_ug = Buf("ug")
        acc = sb("acc", [128, 4, TN], F32)
        B_acc = [Buf("acc%d" % i) for i in range(4)]
        cT = sb("cT", [128, 4, TN], BF16)
        B_cT = [Buf("cT%d" % i) for i in range(4)]
        mT = sb("mT", [128, 8, TN], BF16)
        B_mT = [Buf("mT%d" % i) for i in range(8)]
        qT = sb("qT", [128, 4, TN], BF16)
        B_qT = Buf("qT")
        aoT = sb("aoT", [64, 8, TN], BF16)
        B_ao = [Buf("ao%d" % i) for i in range(8)]
        ktp = mkpool("kt", 3, [128, TN], BF16)
        vtp = mkpool("vt", 3, [128, 4, 128], BF16)
        def hid_f32x2(i):
            return hid[:, 4 * i:4 * i + 4, :].rearrange("p a b -> p (a b)").bitcast(F32).rearrange("p (h n) -> p h n", h=2)
        e_aps, e_bufs = [], []
        for i in range(5):
            e_aps.append(hid_f32x2(i))
            bb = Buf("hf%d" % i)
            for k in range(4):
                alias(bb, B_hid[4 * i + k])
            e_bufs.append(bb)
        ep = Pool("e", e_aps[0:2]); ep.bufs = e_bufs[0:2]
        rsp = Pool("rs", e_aps[2:3]); rsp.bufs = e_bufs[2:3]
        yp = Pool("y", e_aps[3:5]); yp.bufs = e_bufs[3:5]
        lp = mkpool("l2", 3, [128, 2, TN], BF16)
        ap_ = mkpool("a2", 3, [128, 2, TN], BF16)
        ps2 = [es.enter_context(nc.psum_tensor("ps%d" % i, [128, 2, 512], F32)) for i in range(4)]
        psb = [ps2[i // 2][:, i % 2, :] for i in range(8)]
        B_ps = [Buf("ps%d" % i) for i in range(8)]
        psrot = [0]

        def ps_get():
            i = psrot[0] % 4
            psrot[0] += 1
            return psb[i], B_ps[i]

        def ps_fix(i):
            return psb[i], B_ps[i]

        ps6rot = [0]

        def ps6():
            i = ps6rot[0] % 6
            ps6rot[0] += 1
            return psb[i], B_ps[i]

        pieces = [(identf[:], C_ID, 128), (tri[:], C_TRI, 128)] + [(masks[:, kb, :], C_MASK + kb * 512, 512) for kb in range(4)] + [(mmask[:], C_MM, 32)]
        for dst, c0, w in pieces:
            s_ap, s_b = stg.get()
            em.dma("sp", s_ap[:, 0:w], cst_in[:, c0:c0 + w], writes=[s_b])
            em.op("dve", lambda e, dst=dst, s_ap=s_ap, w=w: e.tensor_copy(out=dst, in_=s_ap[:, 0:w]), reads=[s_b], writes=[B_const])
        em.op("dve", lambda e: e.memset(negones[:], -1.0), writes=[B_const])
        em.op("dve", lambda e: e.memset(onesD[:], 1.0 / D), writes=[B_const])
        em.op("dve", lambda e: e.memset(onesC[:], 1.0 / CCH), writes=[B_const])
        em.op("dve", lambda e: e.memset(epsT[:, 0:1], 1e-6), writes=[B_const])
        em.op("dve", lambda e: e.memset(epsT[:, 1:2], 1e-5), writes=[B_const])
        em.dma("sp", pv[:], pvec_in[:, :], writes=[B_const])
        B_flg = Buf("flg")
        em.dma("sp", flg[:], flg_in[:, :], writes=[B_flg])

        B_wpart = Buf("wpart")
        B_wall = Buf("wall")
        wdst = wpart if coll else wall
        for i in range(NWL):
            s_ap, s_b = stg.get()
            em.dma("sp", s_ap[:], wsrc[i * 128:(i + 1) * 128, :], writes=[s_b])
            w_ap, w_b = wpool.get()
            if i % 2 == 1:
                em.op("act", lambda e, w_ap=w_ap, s_ap=s_ap: e.activation(out=w_ap[:], in_=s_ap[:], func=AF.Copy), reads=[s_b], writes=[w_b])
            else:
                em.op("dve", lambda e, w_ap=w_ap, s_ap=s_ap: e.tensor_copy(out=w_ap[:], in_=s_ap[:]), reads=[s_b], writes=[w_b])
            em.dma("sp", wdst.ap()[i * 128:(i + 1) * 128, :], w_ap[:], reads=[w_b], writes=[B_wpart if coll else B_wall], additive=True)
        if coll:
            em.collective("AllGather", [list(range(8))], wpart.ap().opt(), wall.ap().opt(), reads=[], writes=[B_wpart, B_wall])

        def wtile(T):
            w_ap, w_b = wpool.get()
            em.dma("sp", w_ap[:], wall_ap[T * 128:(T + 1) * 128, :], reads=[B_wall], writes=[w_b])
            return w_ap, w_b

        def wstream(Ts, L=3):
            slots = []
            n = len(Ts)
            for i in range(min(L, n)):
                slots.append(wtile(Ts[i]))
            for i in range(n):
                if i + L < n:
                    slots.append(wtile(Ts[i + L]))
                yield slots[i]

        def rmsnorm(N, gcol, out_f32_pool=None):
            p_ap, p_b = ps_get()
            for ic in range(8):
                q_ap, q_b = b16p.get()
                em.op("dve", lambda e, ic=ic, q_ap=q_ap: e.tensor_tensor(out=q_ap[:, :N], in0=hT[:, ic, :N], in1=hT[:, ic, :N], op=ALU.mult),
                      reads=[B_hT[ic]], writes=[q_b])
                em.op("pe", lambda e, ic=ic, q_ap=q_ap: e.matmul(p_ap[:, :N], lhsT=onesD[:], rhs=q_ap[:, :N], start=(ic == 0), stop=(ic == 7)),
                      reads=[q_b, B_const], writes=[p_b])
            l_ap, l_b = f32p.get()
            em.op("act", lambda e: e.activation(out=l_ap[:, :N], in_=p_ap[:, :N], func=AF.Ln, bias=epsT[:, 0:1]), reads=[p_b, B_const], writes=[l_b])
            r_ap, r_b = f32p.get()
            em.op("act", lambda e: e.activation(out=r_ap[:, :N], in_=l_ap[:, :N], func=AF.Exp, scale=-0.5), reads=[l_b], writes=[r_b])
            return r_ap, r_b

        def norm_to_uT(N, gcol):
            r_ap, r_b = rmsnorm(N, gcol)
            for ic in range(8):
                em.op("dve", lambda e, ic=ic: e.scalar_tensor_tensor(out=uT[:, ic, :N], in0=hT[:, ic, :N], scalar=pv[:, gcol + ic:gcol + ic + 1],
                                                                    in1=r_ap[:, :N], op0=ALU.mult, op1=ALU.mult),
                      reads=[B_hT[ic], r_b, B_const], writes=[B_uT[ic]])

        def ffn(N, l, tg, tu, td, ncol):
            norm_to_uT(N, ncol)
            base = l * TPL
            Ts = []
            for fc in range(NFC):
                Ts += [base + tg + fc, base + tu + fc]
            ws = wstream(Ts, L=4)
            for fc in range(NFC):
                g_ap, g_b = next(ws)
                u_ap, u_b = next(ws)
                pg, pg_b = ps_get()
                pu, pu_b = ps_get()

                def mm(e, w=g_ap, p=pg):
                    ins = None
                    for ic in range(8):
                        ins = e.matmul(p[:, :N], lhsT=w[:, ic * 128:(ic + 1) * 128], rhs=uT[:, ic, :N], start=(ic == 0), stop=(ic == 7))
                    return ins
                em.op("pe", mm, reads=[g_b] + B_uT, writes=[pg_b])
                em.op("pe", lambda e: mm(e, u_ap, pu), reads=[u_b] + B_uT, writes=[pu_b])
                s_ap, s_b = f32p.get()
                em.op("act", lambda e: e.activation(out=s_ap[:, :N], in_=pg[:, :N], func=AF.Silu), reads=[pg_b], writes=[s_b])
                em.op("dve", lambda e: e.tensor_tensor(out=hid[:, fc, :N], in0=pu[:, :N], in1=s_ap[:, :N], op=ALU.mult), reads=[pu_b, s_b], writes=[B_hid[fc]])
            for dg in range(2):
                ws = wstream([base + td + fc for fc in range(NFC)], L=4)
                pds = [ps_fix(4 + k) for k in range(4)]
                for fc in range(NFC):
                    d_ap, d_b = next(ws)

                    def mmd(e):
                        ins = None
                        for k in range(4):
                            dc = dg * 4 + k
                            ins = e.matmul(pds[k][0][:, :N], lhsT=d_ap[:, dc * 128:(dc + 1) * 128], rhs=hid[:, fc, :N], start=(fc == 0), stop=(fc == NFC - 1))
                        return ins
                    em.op("pe", mmd, reads=[d_b, B_hid[fc]], writes=[p[1] for p in pds])
                for k in range(4):
                    dc = dg * 4 + k
                    em.op("dve", lambda e, k=k, dc=dc: e.scalar_tensor_tensor(out=hT[:, dc, :N], in0=pds[k][0][:, :N], scalar=0.5, in1=hT[:, dc, :N], op0=ALU.mult, op1=ALU.add),
                          reads=[pds[k][1], B_hT[dc]], writes=[B_hT[dc]])

        B_gin = [Buf("gin%d" % l) for l in range(NLAYER)]
        B_gout = [Buf("gout%d" % l) for l in range(NLAYER)]
        B_loc = [Buf("loc%d" % l) for l in range(NLAYER)]

        def store(dst, src, rb, wb):
            em.dma("sp", dst, src, reads=[rb], writes=[wb], additive=True)

        def mix1(N, l, lc):
            norm_to_uT(N, l * PV_PER_L + PV_MIXN)
            base = l * TPL + T_WIN
            col0 = 0 if lc is None else NMP + lc * TN
            order = list(range(0, 8)) + [12, 16, 13, 17, 14, 18, 15, 19] + list(range(20, 36)) + list(range(8, 12))
            ws = wstream([base + cc for cc in order], L=4)
            pa_hold = None
            gK, gV, gH = gviews(gin[l], 0)
            for cc in order[:-4]:
                w_ap, w_b = next(ws)
                p_ap, p_b = ps_get()

                def mm(e):
                    ins = None
                    for ic in range(8):
                        ins = e.matmul(p_ap[:, :N], lhsT=w_ap[:, ic * 128:(ic + 1) * 128], rhs=uT[:, ic, :N], start=(ic == 0), stop=(ic == 7))
                    return ins
                em.op("pe", mm, reads=[w_b] + B_uT, writes=[p_b])
                if cc < 4:
                    o_ap, o_b = b16p.get()
                    em.op("dve", lambda e: e.tensor_scalar(out=o_ap[:, :N], in0=p_ap[:, :N], scalar1=0.125, scalar2=None, op0=ALU.mult), reads=[p_b], writes=[o_b])
                    store(q_d[l][cc * 128:(cc + 1) * 128, col0:col0 + N], o_ap[:, :N], o_b, B_loc[l])
                elif cc < 8:
                    o_ap, o_b = b16p.get()
                    em.op("dve", lambda e: e.tensor_copy(out=o_ap[:, :N], in_=p_ap[:, :N]), reads=[p_b], writes=[o_b])
                    r0 = (cc - 4) * 128
                    if lc is None:
                        store(km_d[l][r0:r0 + 128, 0:N], o_ap[:, :N], o_b, B_loc[l])
                    else:
                        store(gK[r0:r0 + 128, lc * TN:lc * TN + N], o_ap[:, :N], o_b, B_gin[l])
                elif cc < 16:
                    pa_hold = (p_ap, p_b)
                elif cc < 20:
                    s_ap, s_b = f32p.get()
                    em.op("act", lambda e: e.activation(out=s_ap[:, :N], in_=p_ap[:, :N], func=AF.Sigmoid), reads=[p_b], writes=[s_b])
                    o_ap, o_b = b16p.get()
                    pa, pa_b = pa_hold
                    em.op("dve", lambda e: e.tensor_tensor(out=o_ap[:, :N], in0=pa[:, :N], in1=s_ap[:, :N], op=ALU.mult), reads=[pa_b, s_b], writes=[o_b])
                    c = cc - 16
                    store(ug_d[l][c * 128:(c + 1) * 128, col0:col0 + N], o_ap[:, :N], o_b, B_loc[l])
                    if lc is not None:
                        hrow = gH[lc * 4 + c:lc * 4 + c + 1, :].rearrange("r (p i) -> (r p) i", i=32)
                        store(hrow, o_ap[:, N - 32:N], o_b, B_gin[l])
                else:
                    s_ap, s_b = f32p.get()
                    em.op("act", lambda e: e.activation(out=s_ap[:, :N], in_=p_ap[:, :N], func=AF.Sigmoid), reads=[p_b], writes=[s_b])
                    r0 = (cc - 20) * 128
                    store(gate_d[l][r0:r0 + 128, col0:col0 + N], s_ap[:, :N], s_b, B_loc[l])
            nsub = (N + 127) // 128
            pvs = [ps_fix(4 + k) for k in range(nsub)]
            for cv in range(4):
                w_ap, w_b = next(ws)

                def mmv(e):
                    ins = None
                    for ts in range(nsub):
                        m = min(128, N - ts * 128)
                        for ic in range(8):
                            ins = e.matmul(pvs[ts][0][:m, cv * 128:(cv + 1) * 128], lhsT=uT[:, ic, ts * 128:ts * 128 + m], rhs=w_ap[:, ic * 128:(ic + 1) * 128],
                                           start=(ic == 0), stop=(ic == 7))
                    return ins
                em.op("pe", mmv, reads=[w_b] + B_uT, writes=[p[1] for p in pvs])
            for ts in range(nsub):
                m = min(128, N - ts * 128)
                o_ap, o_b = b16p.get()
                em.op("act", lambda e, ts=ts, m=m: e.activation(out=o_ap[:m, :], in_=pvs[ts][0][:m, :], func=AF.Copy), reads=[pvs[ts][1]], writes=[o_b])
                if lc is None:
                    store(vm_d[l][0:m, :], o_ap[:m, :], o_b, B_loc[l])
                else:
                    t0 = lc * TN + ts * 128
                    store(gV[t0:t0 + m, :], o_ap[:m, :], o_b, B_gin[l])

        def attention(N, l, lc):
            col0 = 0 if lc is None else NMP + lc * TN
            em.dma("sp", qT[:, :, :N], q_d[l][:, col0:col0 + N].rearrange("(c p) t -> p c t", p=128), reads=[B_loc[l]], writes=[B_qT])
            srcs = []
            if lc is not None and coll:
                srcs.append(("own", None, lc))
                srcs.append(("flag", lc % 2, lc))
                for j in range(2 * lc - 1, -1, -1):
                    srcs.append(("prev", owner(j), j // 2))
            elif lc is not None:
                srcs.append(("own", None, lc))
                for j in range(lc - 1, -1, -1):
                    srcs.append(("loc", None, j))
            srcs.append(("meta", None, None))
            for hg in range(4):
                blocks = []
                for (kind, rk, j) in srcs:
                    if kind == "meta":
                        blocks.append((kind, rk, j, 0))
                    else:
                        for kb in range(3, -1, -1):
                            blocks.append((kind, rk, j, kb))
                nblk = len(blocks)
                Ob = [ps_fix(6), ps_fix(7)]
                cur = {}

                def load_src(kind, rk, lcs):
                    kt_ap, kt_b = ktp.get()
                    vt_ap, vt_b = vtp.get()
                    if kind == "meta":
                        em.dma("sp", kt_ap[:, 0:NMP], km_d[l][hg * 128:(hg + 1) * 128, :], reads=[B_loc[l]], writes=[kt_b])
                        em.dma("sp", vt_ap[0:NMP, 0, :], vm_d[l][:, hg * 128:(hg + 1) * 128], reads=[B_loc[l]], writes=[vt_b])
                    else:
                        if kind in ("own", "loc"):
                            (sK, sV, sH), bsrc = gviews(gin[l], 0), B_gin[l]
                        else:
                            (sK, sV, sH), bsrc = gviews(gout[l], rk), B_gout[l]
                        kreg = sK[hg * 128:(hg + 1) * 128, lcs * TN:(lcs + 1) * TN]
                        em.dma("sp", kt_ap[:, :], kreg, reads=[bsrc], writes=[kt_b])
                        vsl = sV[lcs * TN:(lcs + 1) * TN, hg * 128:(hg + 1) * 128].rearrange("(k s) j -> s k j", s=128)
                        em.dma("sp", vt_ap[:, :, :], vsl, reads=[bsrc], writes=[vt_b])
                    return kt_ap, kt_b, vt_ap, vt_b

                work = list(enumerate(blocks))
                st1 = {}
                st2 = {}

                def pair3():
                    i = ps6rot[0] % 3
                    ps6rot[0] += 1
                    return ps2[i], [B_ps[2 * i], B_ps[2 * i + 1]]

                def S1(w):
                    bi, (kind, rk, j, kb) = w
                    key = (kind, rk, j)
                    if cur.get("key") != key:
                        cur["key"] = key
                        cur["t"] = load_src(kind, rk, j)
                    kt_ap, kt_b, vt_ap, vt_b = cur["t"]
                    ks = NMETA if kind == "meta" else 128
                    x_ap, x_bs = pair3()

                    def mmz(e, x_ap=x_ap):
                        ins = None
                        for hh in range(2):
                            hb = hh * 64
                            ins = e.matmul(x_ap[:ks, hh, :N], lhsT=kt_ap[hb:hb + 64, kb * 128:kb * 128 + ks], rhs=qT[hb:hb + 64, hg, :N], start=True, stop=True)
                        return ins
                    em.op("pe", mmz, reads=[kt_b, B_qT], writes=x_bs)
                    e_ap, e_b = ep.get()
                    if kind == "flag":
                        em.op("act", lambda e: e.activation(out=e_ap[:ks, :, :N], in_=x_ap[:ks, :, :N], func=AF.Exp, bias=flg[:ks, lc:lc + 1]), reads=x_bs + [B_flg], writes=[e_b])
                    else:
                        em.op("act", lambda e: e.activation(out=e_ap[:ks, :, :N], in_=x_ap[:ks, :, :N], func=AF.Exp), reads=x_bs, writes=[e_b])
                    l_ap, l_b = lp.get()
                    em.op("act", lambda e: e.activation(out=l_ap[:ks, :, :N], in_=e_ap[:ks, :, :N], func=AF.Ln, bias=1.0), reads=[e_b], writes=[l_b])
                    mk = None
                    if kind == "own":
                        mk = masks[:, kb, :N]
                    elif kind == "meta" and lc is None:
                        mk = mmask[:, :N]
                    if mk is not None:
                        for hh in range(2):
                            em.op("dve", lambda e, hh=hh: e.tensor_tensor(out=l_ap[:ks, hh, :N], in0=l_ap[:ks, hh, :N], in1=mk[:ks, :], op=ALU.mult), reads=[l_b, B_const], writes=[l_b])
                    st1[bi] = (kt_ap, kt_b, l_ap, l_b, vt_ap, vt_b, ks, mk)

                def S2(w):
                    bi, (kind, rk, j, kb) = w
                    kt_ap, kt_b, l_ap, l_b, vt_ap, vt_b, ks, mk = st1.pop(bi)
                    first = (bi == 0)
                    last = (bi == nblk - 1)
                    rs_ap, rs_b = rsp.aps[0], rsp.bufs[0]
                    x_ap, x_bs = pair3()

                    def mmx(e):
                        ins = None
                        for hh in range(2):
                            hb = hh * 64
                            e.matmul(x_ap[:ks, hh, :N], lhsT=kt_ap[hb:hb + 64, kb * 128:kb * 128 + ks], rhs=qT[hb:hb + 64, hg, :N], start=True, stop=False)
                            ins = e.matmul(x_ap[:ks, hh, :N], lhsT=tri[0:ks, 0:ks], rhs=l_ap[:ks, hh, :N], start=False, stop=True)
                        return ins
                    em.op("pe", mmx, reads=[kt_b, B_qT, l_b, B_const], writes=x_bs)
                    if not last:
                        rc_ap, rc_bs = pair3()

                        def mmr(e):
                            ins = None
                            for hh in range(2):
                                ins = e.matmul(rc_ap[:, hh, :N], lhsT=negones[0:ks, :], rhs=l_ap[:ks, hh, :N], start=True, stop=True)
                            return ins
                        em.op("pe", mmr, reads=[l_b, B_const], writes=rc_bs)
                    a_ap, a_b = ap_.get()
                    kw = {}
                    rd = []
                    if kind == "flag":
                        kw = dict(bias=flg[:ks, lc:lc + 1])
                        rd = [B_flg]
                    if first:
                        em.op("act", lambda e: e.activation(out=a_ap[:ks, :, :N], in_=x_ap[:ks, :, :N], func=AF.Exp, **kw), reads=x_bs + rd, writes=[a_b])
                        if not last:
                            em.op("dve", lambda e: e.tensor_copy(out=rs_ap[:, :, :N], in_=rc_ap[:, :, :N]), reads=rc_bs, writes=[rs_b])
                    else:
                        y_ap, y_b = yp.get()
                        em.op("dve", lambda e: e.tensor_tensor(out=y_ap[:ks, :, :N], in0=x_ap[:ks, :, :N], in1=rs_ap[:ks, :, :N], op=ALU.add), reads=x_bs + [rs_b], writes=[y_b])
                        em.op("act", lambda e: e.activation(out=a_ap[:ks, :, :N], in_=y_ap[:ks, :, :N], func=AF.Exp, **kw), reads=[y_b] + rd, writes=[a_b])
                        if not last:
                            em.op("dve", lambda e: e.tensor_tensor(out=rs_ap[:, :, :N], in0=rc_ap[:, :, :N], in1=rs_ap[:, :, :N], op=ALU.add), reads=rc_bs + [rs_b], writes=[rs_b])
                    if mk is not None:
                        for hh in range(2):
                            em.op("dve", lambda e, hh=hh: e.tensor_tensor(out=a_ap[:ks, hh, :N], in0=a_ap[:ks, hh, :N], in1=mk[:ks, :], op=ALU.mult), reads=[a_b, B_const], writes=[a_b])
                    st2[bi] = (a_ap, a_b, vt_ap, vt_b, ks, first, last)

                def S3(w):
                    bi, (kind, rk, j, kb) = w
                    a_ap, a_b, vt_ap, vt_b, ks, first, last = st2.pop(bi)

                    def mmo(e):
                        ins = None
                        for hh in range(2):
                            ins = e.matmul(Ob[hh][0][0:64, :N], lhsT=vt_ap[0:ks, kb, hh * 64:(hh + 1) * 64], rhs=a_ap[:ks, hh, :N], start=first, stop=last)
                        return ins
                    em.op("pe", mmo, reads=[a_b, vt_b], writes=[Ob[0][1], Ob[1][1]])
                    if last:
                        for hh in range(2):
                            h = hg * 2 + hh
                            em.op("act", lambda e, hh=hh, h=h: e.activation(out=aoT[0:64, h, :N], in_=Ob[hh][0][0:64, :N], func=AF.Copy), reads=[Ob[hh][1]], writes=[B_ao[h]])

                nw = len(work)
                for step in range(nw + 2):
                    if step < nw:
                        S1(work[step])
                    if 0 <= step - 1 < nw:
                        S2(work[step - 1])
                    if 0 <= step - 2 < nw:
                        S3(work[step - 2])

        def mix2(N, l, lc):
            col0 = 0 if lc is None else NMP + lc * TN
            pvl = l * PV_PER_L
            em.dma("sp", ug[:, :, 32:32 + N], ug_d[l][:, col0:col0 + N].rearrange("(c p) t -> p c t", p=128), reads=[B_loc[l]], writes=[B_ug])
            if lc is None:
                em.op("dve", lambda e: e.memset(ug[:, :, 0:32], 0.0), writes=[B_ug])
            else:
                if not coll:
                    if lc == 0:
                        em.dma("sp", h2t[:, :, :], ug_d[l][:, 0:NMP].rearrange("(c p) t -> p c t", p=128), reads=[B_loc[l]], writes=[B_h2])
                        em.op("dve", lambda e: e.memset(ug[:, :, 0:16], 0.0), writes=[B_ug])
                        em.op("dve", lambda e: e.tensor_copy(out=ug[:, :, 16:32], in_=h2t[:, :, 0:16]), reads=[B_h2, B_ug], writes=[B_ug])
                    else:
                        hv = gviews(gin[l], 0)[2]
                        r0 = (lc - 1) * 4
                        em.dma("sp", ug[:, :, 0:32], hv[r0:r0 + 4, :].rearrange("c (p i) -> p c i", i=32), reads=[B_gin[l], B_ug], writes=[B_ug])
                else:
                    if lc == 0:
                        em.dma("sp", h2t[:, :, :], ug_d[l][:, 0:NMP].rearrange("(c p) t -> p c t", p=128), reads=[B_loc[l]], writes=[B_h2])
                        em.op("dve", lambda e: e.memset(h1t[:, :, 0:16], 0.0), writes=[B_h1])
                        em.op("dve", lambda e: e.tensor_copy(out=h1t[:, :, 16:32], in_=h2t[:, :, 0:16]), reads=[B_h2, B_h1], writes=[B_h1])
                    else:
                        jj = 2 * lc - 1
                        hv = gviews(gout[l], owner(jj))[2]
                        r0 = (jj // 2) * 4
                        em.dma("sp", h1t[:, :, :], hv[r0:r0 + 4, :].rearrange("c (p i) -> p c i", i=32), reads=[B_gout[l]], writes=[B_h1])
                    hv = gviews(gout[l], lc % 2)[2]
                    r0 = lc * 4
                    em.dma("sp", h2t[:, :, :], hv[r0:r0 + 4, :].rearrange("c (p i) -> p c i", i=32), reads=[B_gout[l]], writes=[B_h2])
                    em.op("dve", lambda e: e.tensor_scalar(out=h3t[:], in0=h1t[:], scalar1=flg[:, 16 + lc:17 + lc], scalar2=None, op0=ALU.mult), reads=[B_h1, B_flg], writes=[B_h3])
                    em.op("dve", lambda e: e.scalar_tensor_tensor(out=ug[:, :, 0:32], in0=h2t[:], scalar=flg[:, 8 + lc:9 + lc], in1=h3t[:], op0=ALU.mult, op1=ALU.add),
                          reads=[B_h2, B_h3, B_flg], writes=[B_ug])
            for c in range(4):
                cw = pvl + PV_CW
                em.op("dve", lambda e, c=c: e.tensor_scalar(out=acc[:, c, :N], in0=ug[:, c, 2:2 + N], scalar1=pv[:, cw + c:cw + c + 1],
                                                         scalar2=pv[:, pvl + PV_CB + c:pvl + PV_CB + c + 1], op0=ALU.mult, op1=ALU.add),
                      reads=[B_ug, B_const], writes=[B_acc[c]])

                for k in range(1, KC):
                    em.op("dve", lambda e, c=c, k=k: e.scalar_tensor_tensor(out=acc[:, c, :N], in0=ug[:, c, 2 + k:2 + k + N], scalar=pv[:, cw + 4 * k + c:cw + 4 * k + c + 1],
                                                                          in1=acc[:, c, :N], op0=ALU.mult, op1=ALU.add),
                          reads=[B_ug, B_const, B_acc[c]], writes=[B_acc[c]])
            pm, pm_b = ps_fix(4)
            pq, pq_b = ps_fix(5)
            for c in range(4):
                yb, yb_b = b16p.get()
                em.op("act", lambda e, c=c: e.activation(out=yb[:, :N], in_=acc[:, c, :N], func=AF.Copy), reads=[B_acc[c]], writes=[yb_b])
                em.op("pe", lambda e, c=c: e.matmul(pm[:, :N], lhsT=onesC[:], rhs=yb[:, :N], start=(c == 0), stop=(c == 3)), reads=[yb_b, B_const], writes=[pm_b])
                ys, ys_b = b16p.get()
                em.op("act", lambda e, c=c: e.activation(out=ys[:, :N], in_=acc[:, c, :N], func=AF.Square), reads=[B_acc[c]], writes=[ys_b])
                em.op("pe", lambda e, c=c: e.matmul(pq[:, :N], lhsT=onesC[:], rhs=ys[:, :N], start=(c == 0), stop=(c == 3)), reads=[ys_b, B_const], writes=[pq_b])
            mean, mean_b = f32p.get()
            em.op("act", lambda e: e.activation(out=mean[:, :N], in_=pm[:, :N], func=AF.Copy), reads=[pm_b], writes=[mean_b])
            msq, msq_b = f32p.get()
            em.op("dve", lambda e: e.tensor_tensor(out=msq[:, :N], in0=mean[:, :N], in1=mean[:, :N], op=ALU.mult), reads=[mean_b], writes=[msq_b])
            var, var_b = f32p.get()
            em.op("dve", lambda e: e.tensor_tensor(out=var[:, :N], in0=pq[:, :N], in1=msq[:, :N], op=ALU.subtract), reads=[pq_b, msq_b], writes=[var_b])
            em.op("dve", lambda e: e.tensor_scalar(out=var[:, :N], in0=var[:, :N], scalar1=0.0, scalar2=None, op0=ALU.max), reads=[var_b], writes=[var_b])
            lnv, lnv_b = f32p.get()
            em.op("act", lambda e: e.activation(out=lnv[:, :N], in_=var[:, :N], func=AF.Ln, bias=epsT[:, 1:2]), reads=[var_b, B_const], writes=[lnv_b])
            rstd, rstd_b = f32p.get()
            em.op("act", lambda e: e.activation(out=rstd[:, :N], in_=lnv[:, :N], func=AF.Exp, scale=-0.5), reads=[lnv_b], writes=[rstd_b])
            for c in range(4):
                em.op("dve", lambda e, c=c: e.tensor_tensor(out=acc[:, c, :N], in0=acc[:, c, :N], in1=mean[:, :N], op=ALU.subtract), reads=[B_acc[c], mean_b], writes=[B_acc[c]])
                em.op("dve", lambda e, c=c: e.tensor_tensor(out=acc[:, c, :N], in0=acc[:, c, :N], in1=rstd[:, :N], op=ALU.mult), reads=[B_acc[c], rstd_b], writes=[B_acc[c]])
                em.op("act", lambda e, c=c: e.activation(out=cT[:, c, :N], in_=acc[:, c, :N], func=AF.Silu, scale=pv[:, pvl + PV_LNG + c:pvl + PV_LNG + c + 1],
                                                       bias=pv[:, pvl + PV_LNB + c:pvl + PV_LNB + c + 1]),
                      reads=[B_acc[c], B_const], writes=[B_cT[c]])
            attention(N, l, lc)
            base = l * TPL
            Ts = []
            for dc in range(8):
                Ts += [base + T_WAO + dc, base + T_WCO + dc]
            ws = wstream(Ts, L=4)
            for dc in range(8):
                wa, wa_b = next(ws)
                wc, wc_b = next(ws)
                gs, gs_b = f32p.get()
                gc, gc_b = f32p.get()
                em.dma("sp", gs[:, :N], gate_d[l][dc * 128:(dc + 1) * 128, col0:col0 + N], reads=[B_loc[l]], writes=[gs_b])
                em.dma("sp", gc[:, :N], gate_d[l][D + dc * 128:D + (dc + 1) * 128, col0:col0 + N], reads=[B_loc[l]], writes=[gc_b])
                p1, p1_b = ps_get()
                p2, p2_b = ps_get()

                def mm1(e):
                    ins = None
                    for h in range(8):
                        ins = e.matmul(p1[:, :N], lhsT=wa[0:64, h * 128:(h + 1) * 128], rhs=aoT[0:64, h, :N], start=(h == 0), stop=(h == 7))
                    return ins
                em.op("pe", mm1, reads=[wa_b] + B_ao, writes=[p1_b])

                def mm2(e):
                    ins = None
                    for c in range(4):
                        ins = e.matmul(p2[:, :N], lhsT=wc[:, c * 128:(c + 1) * 128], rhs=cT[:, c, :N], start=(c == 0), stop=(c == 3))
                    return ins
                em.op("pe", mm2, reads=[wc_b] + B_cT, writes=[p2_b])
                t1, t1_b = f32p.get()
                em.op("dve", lambda e: e.tensor_tensor(out=t1[:, :N], in0=p1[:, :N], in1=gs[:, :N], op=ALU.mult), reads=[p1_b, gs_b], writes=[t1_b])
                t2, t2_b = f32p.get()
                em.op("dve", lambda e: e.tensor_tensor(out=t2[:, :N], in0=p2[:, :N], in1=gc[:, :N], op=ALU.mult), reads=[p2_b, gc_b], writes=[t2_b])
                em.op("dve", lambda e: e.tensor_tensor(out=mT[:, dc, :N], in0=t1[:, :N], in1=t2[:, :N], op=ALU.add), reads=[t1_b, t2_b], writes=[B_mT[dc]])
            ws = wstream([base + T_WOUT + dc for dc in range(8)], L=4)
            for dc in range(8):
                wo, wo_b = next(ws)
                po, po_b = ps_get()

                def mmo(e):
                    ins = None
                    for ic in range(8):
                        ins = e.matmul(po[:, :N], lhsT=wo[:, ic * 128:(ic + 1) * 128], rhs=mT[:, ic, :N], start=(ic == 0), stop=(ic == 7))
                    return ins
                em.op("pe", mmo, reads=[wo_b] + B_mT, writes=[po_b])
                em.op("dve", lambda e: e.tensor_tensor(out=hT[:, dc, :N], in0=po[:, :N], in1=hT[:, dc, :N], op=ALU.add), reads=[po_b, B_hT[dc]], writes=[B_hT[dc]])

        B_hTd = {}

        def tile_cols(lc):
            return (0, NMP) if lc is None else (NMP + lc * TN, TN)

        def load_x(lc):
            N = NMP if lc is None else TN
            nsub = (N + 127) // 128
            tiles = []
            for ts in range(nsub):
                m = min(128, N - ts * 128)
                s_ap, s_b = stg.get()
                if lc is None:
                    em.op("dve", lambda e: e.memset(s_ap[0:NMP, :], 0.0), writes=[s_b])
                    em.dma("sp", s_ap[0:NMETA, :], meta_in[0:NMETA, :], reads=[s_b], writes=[s_b])
                else:
                    em.dma("sp", s_ap[:m, :], x_in[lc * TN + ts * 128:lc * TN + ts * 128 + m, :], writes=[s_b])
                tiles.append((s_ap, s_b, m))
                if len(tiles) == 2 or ts == nsub - 1:
                    for ic in range(8):
                        p_ap, p_b = ps_get()

                        def tr(e):
                            ins = None
                            for i, (sa, sbb, mm_) in enumerate(tiles):
                                ins = e.transpose(out=p_ap[:, i * 128:i * 128 + mm_], in_=sa[:mm_, ic * 128:(ic + 1) * 128], identity=identf[:mm_, :mm_])
                            return ins
                        em.op("pe", tr, reads=[t[1] for t in tiles] + [B_const], writes=[p_b])
                        t0 = (ts - len(tiles) + 1) * 128
                        w = sum(t[2] for t in tiles)
                        em.op("act", lambda e, ic=ic, t0=t0, w=w: e.activation(out=hT[:, ic, t0:t0 + w], in_=p_ap[:, 0:w], func=AF.Copy), reads=[p_b], writes=[B_hT[ic]])
                    tiles = []

        def load_h(lc):
            c0, N = tile_cols(lc)
            em.dma("sp", hT[:, :, :N], hT_d[:, c0:c0 + N].rearrange("(c p) t -> p c t", p=128), reads=[B_hTd[lc]], writes=B_hT)

        def store_h(lc):
            c0, N = tile_cols(lc)
            if lc not in B_hTd:
                B_hTd[lc] = Buf("hTd")
            em.dma("sp", hT_d[:, c0:c0 + N].rearrange("(c p) t -> p c t", p=128), hT[:, :, :N], reads=B_hT, writes=[B_hTd[lc]])

        B_y = Buf("y")

        def final_out(lc):
            N = TN
            r_ap, r_b = rmsnorm(N, PV_FINAL)
            for ic in range(8):
                em.op("dve", lambda e, ic=ic: e.scalar_tensor_tensor(out=hT[:, ic, :N], in0=hT[:, ic, :N], scalar=pv[:, PV_FINAL + ic:PV_FINAL + ic + 1],
                                                                    in1=r_ap[:, :N], op0=ALU.mult, op1=ALU.mult),
                      reads=[B_hT[ic], r_b, B_const], writes=[B_hT[ic]])
            for ts in range(4):
                s_ap, s_b = stg.get()
                for half in range(2):
                    p_ap, p_b = ps_get()

                    def tr(e):
                        ins = None
                        for i in range(4):
                            ic = half * 4 + i
                            ins = e.transpose(out=p_ap[:, i * 128:(i + 1) * 128], in_=hT[:, ic, ts * 128:(ts + 1) * 128], identity=identf[:])
                        return ins
                    em.op("pe", tr, reads=B_hT[half * 4:half * 4 + 4] + [B_const], writes=[p_b])
                    em.op("act", lambda e, half=half: e.activation(out=s_ap[:, half * 512:(half + 1) * 512], in_=p_ap[:, :], func=AF.Copy), reads=[p_b], writes=[s_b])
                em.dma("sp", y_out[lc * TN + ts * 128:lc * TN + (ts + 1) * 128, :], s_ap[:, :], reads=[s_b], writes=[B_y], additive=True)

        pairs = [[0, 1], [2, 3], [4, 5], [6, 7]]
        tiles = [None] + list(range(NCH))

        def tn(lc):
            return NMP if lc is None else TN

        for lc in tiles:
            if stage < 1:
                break
            load_x(lc)
            if stage >= 2:
                ffn(tn(lc), 0, T_F1G, T_F1U, T_F1D, 0 * PV_PER_L + PV_F1N)
            if stage >= 3:
                mix1(tn(lc), 0, lc)
            store_h(lc)
        if stage >= 4 and coll:
            em.collective("AllGather", pairs, gin[0].ap().opt(), gout[0].ap().opt(), reads=[], writes=[B_gin[0], B_gout[0]])
        if stage < 5:
            if stage >= 1:
                em.dma("sp", y_out[0:128, 0:NT].rearrange("(o p) t -> p o t", o=1) if False else y_out[0:128, 0:512], hT_d[0:128, NMP:NMP + 512], reads=[B_hTd[0]], writes=[B_y], additive=True)
            em.barrier_end([B_y, B_wall])
            return nc, em
        for lc in tiles:
            load_h(lc)
            mix2(tn(lc), 0, lc)
            ffn(tn(lc), 0, T_F2G, T_F2U, T_F2D, 0 * PV_PER_L + PV_F2N)
            ffn(tn(lc), 1, T_F1G, T_F1U, T_F1D, 1 * PV_PER_L + PV_F1N)
            mix1(tn(lc), 1, lc)
            store_h(lc)
        if coll:
            em.collective("AllGather", pairs, gin[1].ap().opt(), gout[1].ap().opt(), reads=[], writes=[B_gin[1], B_gout[1]])
        for lc in tiles[1:]:
            load_h(lc)
            mix2(TN, 1, lc)
            ffn(TN, 1, T_F2G, T_F2U, T_F2D, 1 * PV_PER_L + PV_F2N)
            final_out(lc)
        em.barrier_end([B_y])
    return nc, em


def chunk_of(rank, lc):
    return 2 * lc + (rank if lc % 2 == 0 else 1 - rank)


def host_pack(inputs, NCH, coll=True):
    f = lambda k: np.asarray(inputs[k], dtype=np.float32)
    wt = np.zeros((NWT, 128, 1024), np.float32)

    def coltile(W, cc):
        K = W.shape[0]
        blk = W[:, cc * 128:(cc + 1) * 128].reshape(K // 128, 128, 128).transpose(1, 0, 2).reshape(128, K)
        t = np.zeros((128, 1024), np.float32)
        t[:, :K] = blk
        return t

    for l in range(NLAYER):
        b = l * TPL
        for fc in range(NFC):
            wt[b + T_F1G + fc] = coltile(f("ffn1_w_gate")[l], fc)
            wt[b + T_F1U + fc] = coltile(f("ffn1_w_up")[l], fc)
            wt[b + T_F1D + fc] = f("ffn1_w_down")[l][fc * 128:(fc + 1) * 128, :]
            wt[b + T_F2G + fc] = coltile(f("ffn2_w_gate")[l], fc)
            wt[b + T_F2U + fc] = coltile(f("ffn2_w_up")[l], fc)
            wt[b + T_F2D + fc] = f("ffn2_w_down")[l][fc * 128:(fc + 1) * 128, :]
        for cc in range(36):
            wt[b + T_WIN + cc] = coltile(f("w_in")[l], cc)
        for dc in range(8):
            W = f("w_attn_out")[l]
            wt[b + T_WAO + dc][0:64, :] = W[:, dc * 128:(dc + 1) * 128].reshape(8, 64, 128).transpose(1, 0, 2).reshape(64, 1024)
            wt[b + T_WCO + dc] = coltile(f("w_conv_out")[l], dc)
            wt[b + T_WOUT + dc] = coltile(f("w_out")[l], dc)
    pvec = np.zeros((128, PV_N), np.float32)

    def put(col, v):
        n = v.shape[0] // 128
        pvec[:, col:col + n] = v.reshape(n, 128).T

    for l in range(NLAYER):
        b = l * PV_PER_L
        put(b + PV_F1N, f("ffn1_norm")[l]); put(b + PV_MIXN, f("mix_norm")[l]); put(b + PV_F2N, f("ffn2_norm")[l])
        put(b + PV_CB, f("conv_b")[l]); put(b + PV_LNG, f("conv_ln_g")[l]); put(b + PV_LNB, f("conv_ln_b")[l])
        cw = f("conv_w")[l]
        for k in range(KC):
            put(b + PV_CW + 4 * k, cw[k])
    put(PV_FINAL, f("final_norm"))
    cst = np.zeros((128, C_N), np.float32)
    cst[:, C_ID:C_ID + 128] = np.eye(128, dtype=np.float32)
    jj, ss = np.meshgrid(np.arange(128), np.arange(128), indexing="ij")
    cst[:, C_TRI:C_TRI + 128] = np.where(jj >= ss, -1.0, 0.0)
    tt = np.arange(512)[None, :]
    for kb in range(4):
        cst[:, C_MASK + kb * 512:C_MASK + (kb + 1) * 512] = ((kb * 128 + np.arange(128)[:, None]) < tt).astype(np.float32)
    cst[0:16, C_MM:C_MM + 32] = (np.arange(16)[:, None] < np.arange(32)[None, :]).astype(np.float32)
    x = f("x")
    meta = f("meta")
    if not coll:
        wflat = np.ascontiguousarray(wt.reshape(NWT * 128, 1024))
        flg0 = np.zeros((128, 24), np.float32)
        return [{"x": np.ascontiguousarray(x[b, :NCH * TN]), "meta": meta, "wsrc": wflat, "pvec": pvec, "cst": cst, "flg": flg0} for b in range(4)]
    in_maps = []
    for c in range(8):
        b, rank = c // 2, c % 2
        xs = np.concatenate([x[b, chunk_of(rank, lc) * TN:(chunk_of(rank, lc) + 1) * TN] for lc in range(NCH)], axis=0)
        flg = np.zeros((128, 24), np.float32)
        for lc in range(NCH):
            later = (rank != lc % 2)
            flg[:, lc] = 0.0 if later else -30000.0
            flg[:, 8 + lc] = 1.0 if later else 0.0
            flg[:, 16 + lc] = 0.0 if later else 1.0
        in_maps.append({"x": np.ascontiguousarray(xs), "meta": meta,
                        "wsrc": np.ascontiguousarray(wt[c * WT_PER_CORE:(c + 1) * WT_PER_CORE].reshape(WT_PER_CORE * 128, 1024)),
                        "pvec": pvec, "cst": cst, "flg": flg})
    return in_maps


_CACHE = {}


def run_nocoll(inputs, NCH):
    key = ("nocoll", NCH)
    if key not in _CACHE:
        _CACHE[key] = build(NCH, coll=False)
    nc, em = _CACHE[key]
    in_maps = host_pack(inputs, NCH, coll=False)
    res = run_bass_kernel_spmd(nc, in_maps, core_ids=list(range(4)))
    return np.stack([np.asarray(res.results[b]["y"]) for b in range(4)], axis=0)


def run(inputs, NCH):
    if NCH not in _CACHE:
        _CACHE[NCH] = build(NCH)
    nc, em = _CACHE[NCH]
    in_maps = host_pack(inputs, NCH)
    res = run_bass_kernel_spmd(nc, in_maps, core_ids=list(range(8)))
    B = 4
    out = np.zeros((B, 2 * NCH * TN, D), np.float32)
    for c in range(8):
        b, rank = c // 2, c % 2
        y = np.asarray(res.results[c]["y"])
        for lc in range(NCH):
            J = chunk_of(rank, lc)
            out[b, J * TN:(J + 1) * TN] = y[lc * TN:(lc + 1) * TN]
    return out


USE_COLL = False


def kernel(**inputs):
    if USE_COLL:
        return run(inputs, NCH_FULL)
    return run_nocoll(inputs, 2 * NCH_FULL)
```
